# Optimizing a Trainium2 kernel written in Bass

```python
import math
import jax, jax.numpy as jnp
from jax import lax
import numpy as np

D_MODEL = 2048
BATCH = 1
SEQ = 16384
DEPTH = 1

CHUNK = 64
PLE_DIM = 256
D_FF = 5632
MIX_WIDTH = D_MODEL
POOL_WIDTH = MIX_WIDTH // 2
POOL_GROUPS = 4
POOL_GROUP_DIM = POOL_WIDTH // POOL_GROUPS
POOL_WINDOWS = (2, 4, 8, 16)
N_HEADS = 8
HEAD_DIM = 128
ATTN_WIDTH = N_HEADS * HEAD_DIM
LEFT_CHUNKS = 8
LEFT = LEFT_CHUNKS * CHUNK
BAND = (LEFT_CHUNKS + 1) * CHUNK
REL_CLIP = 256
IN_COLS = POOL_WIDTH + 3 * ATTN_WIDTH
N_BRANCHES = 2
EPS = 1e-6
MASK_VALUE = -1e30

kernel_name = "hybrid_pool_chunkattn_macaron_block"


def rms_norm(x, g):
    xf = x.astype(jnp.float32)
    y = xf * lax.rsqrt(jnp.mean(xf * xf, axis=-1, keepdims=True) + EPS)
    return (y * g.astype(jnp.float32)).astype(x.dtype)


def swiglu(x, w_gate, w_up, w_down):
    return (jax.nn.silu(x @ w_gate) * (x @ w_up)) @ w_down


def pool_mixer(z, group_w, scale):
    b, s, _ = z.shape
    zg = z.reshape(b, s, POOL_GROUPS, POOL_GROUP_DIM)
    zf = zg.astype(jnp.float32)
    cs = jnp.cumsum(zf, axis=1)
    cs_pad = jnp.pad(cs, ((0, 0), (1, 0), (0, 0), (0, 0)))
    t = jnp.arange(s, dtype=jnp.int32)
    pooled = []
    for g, w in enumerate(POOL_WINDOWS):
        upper = cs_pad[:, 1:, g]
        lower = jnp.pad(cs_pad[:, : s - w + 1, g], ((0, 0), (w - 1, 0), (0, 0)))
        count = jnp.minimum(t + 1, w).astype(jnp.float32)[None, :, None]
        pooled.append((upper - lower) / count)
    pooled = jnp.stack(pooled, axis=2)
    diff = (pooled - zf).astype(z.dtype)
    y = jnp.einsum('bsgc,gcd->bsgd', diff, group_w)
    return y.reshape(b, s, POOL_WIDTH) * scale


def chunk_attention(q, k, v, q_gain, k_gain, rel_bias):
    b, s, _ = q.shape
    nc = s // CHUNK
    q = rms_norm(q.reshape(b, s, N_HEADS, HEAD_DIM), q_gain) * (HEAD_DIM ** -0.5)
    k = rms_norm(k.reshape(b, s, N_HEADS, HEAD_DIM), k_gain)
    v = v.reshape(b, s, N_HEADS, HEAD_DIM)
    q = q.transpose(0, 2, 1, 3).reshape(b, N_HEADS, nc, CHUNK, HEAD_DIM)
    q_chunks = jnp.moveaxis(q, 2, 0)
    k_pad = jnp.pad(k.transpose(0, 2, 1, 3), ((0, 0), (0, 0), (LEFT, 0), (0, 0)))
    v_pad = jnp.pad(v.transpose(0, 2, 1, 3), ((0, 0), (0, 0), (LEFT, 0), (0, 0)))
    qi = jnp.arange(CHUNK, dtype=jnp.int32)[:, None]
    kj = jnp.arange(BAND, dtype=jnp.int32)[None, :]
    rel_idx = jnp.clip(qi - kj + LEFT, -REL_CLIP, REL_CLIP) + REL_CLIP
    bias = rel_bias[:, rel_idx].astype(jnp.float32)

    def attend(args):
        c, q_c = args
        start = c * CHUNK
        k_band = lax.dynamic_slice_in_dim(k_pad, start, BAND, axis=2)
        v_band = lax.dynamic_slice_in_dim(v_pad, start, BAND, axis=2)
        sc = jnp.einsum('bhqd,bhkd->bhqk', q_c, k_band).astype(jnp.float32) + bias
        key_pos = start - LEFT + jnp.arange(BAND, dtype=jnp.int32)
        sc = jnp.where((key_pos >= 0)[None, None, None, :], sc, MASK_VALUE)
        pr = jax.nn.softmax(sc, axis=-1).astype(v_band.dtype)
        return jnp.einsum('bhqk,bhkd->bhqd', pr, v_band)

    out = lax.map(attend, (jnp.arange(nc, dtype=jnp.int32), q_chunks))
    out = out.transpose(1, 0, 3, 2, 4).reshape(b, s, ATTN_WIDTH)
    return out


def setup_inputs(seed: int = 0) -> dict:
    key = jax.random.key(seed)
    ks = jax.random.split(key, 32)
    f32 = jnp.float32

    def w(k, shape, fan_in):
        return jax.random.normal(k, shape, f32) * (fan_in ** -0.5)

    def gain(k, shape):
        return 1.0 + 0.05 * jax.random.normal(k, shape, f32)

    return {
        "x": jax.random.normal(ks[0], (BATCH, SEQ, D_MODEL), f32),
        "p": jax.random.normal(ks[1], (DEPTH, BATCH, SEQ, PLE_DIM), f32),
        "ffn1_norm": gain(ks[2], (DEPTH, D_MODEL)),
        "ffn1_w_gate": w(ks[3], (DEPTH, D_MODEL, D_FF), D_MODEL),
        "ffn1_w_up": w(ks[4], (DEPTH, D_MODEL, D_FF), D_MODEL),
        "ffn1_w_down": w(ks[5], (DEPTH, D_FF, D_MODEL), D_FF),
        "mix_norm": gain(ks[6], (DEPTH, D_MODEL)),
        "w_in": w(ks[7], (DEPTH, D_MODEL, IN_COLS), D_MODEL),
        "pool_w": w(ks[8], (DEPTH, POOL_GROUPS, POOL_GROUP_DIM, POOL_GROUP_DIM), POOL_GROUP_DIM),
        "pool_scale": gain(ks[9], (DEPTH, POOL_WIDTH)),
        "q_norm": gain(ks[10], (DEPTH, HEAD_DIM)),
        "k_norm": gain(ks[11], (DEPTH, HEAD_DIM)),
        "rel_bias": 0.1 * jax.random.normal(ks[12], (DEPTH, N_HEADS, 2 * REL_CLIP + 1), f32),
        "w_br_pool": w(ks[13], (DEPTH, POOL_WIDTH, D_MODEL), POOL_WIDTH),
        "w_br_attn": w(ks[14], (DEPTH, ATTN_WIDTH, D_MODEL), ATTN_WIDTH),
        "w_branch_gate": w(ks[15], (DEPTH, D_MODEL, N_BRANCHES * D_MODEL), D_MODEL),
        "b_branch_gate": 0.02 * jax.random.normal(ks[16], (DEPTH, N_BRANCHES * D_MODEL), f32),
        "w_out": w(ks[17], (DEPTH, D_MODEL, D_MODEL), D_MODEL),
        "ffn2_norm": gain(ks[18], (DEPTH, D_MODEL)),
        "ffn2_w_gate": w(ks[19], (DEPTH, D_MODEL, D_FF), D_MODEL),
        "ffn2_w_up": w(ks[20], (DEPTH, D_MODEL, D_FF), D_MODEL),
        "ffn2_w_down": w(ks[21], (DEPTH, D_FF, D_MODEL), D_FF),
        "ple_norm": gain(ks[22], (DEPTH, D_MODEL)),
        "w_ple_gate": w(ks[23], (DEPTH, D_MODEL, D_MODEL), D_MODEL),
        "w_ple": w(ks[24], (DEPTH, PLE_DIM, D_MODEL), PLE_DIM),
    }


def reference(x, p, ffn1_norm, ffn1_w_gate, ffn1_w_up, ffn1_w_down, mix_norm, w_in,
              pool_w, pool_scale, q_norm, k_norm, rel_bias, w_br_pool, w_br_attn,
              w_branch_gate, b_branch_gate, w_out, ffn2_norm, ffn2_w_gate, ffn2_w_up,
              ffn2_w_down, ple_norm, w_ple_gate, w_ple):
    h = x
    for i in range(DEPTH):
        h = h + 0.5 * swiglu(rms_norm(h, ffn1_norm[i]), ffn1_w_gate[i], ffn1_w_up[i], ffn1_w_down[i])
        u = rms_norm(h, mix_norm[i])
        proj = u @ w_in[i]
        z_pool = proj[..., :POOL_WIDTH]
        q = proj[..., POOL_WIDTH:POOL_WIDTH + ATTN_WIDTH]
        k = proj[..., POOL_WIDTH + ATTN_WIDTH:POOL_WIDTH + 2 * ATTN_WIDTH]
        v = proj[..., POOL_WIDTH + 2 * ATTN_WIDTH:]
        y_pool = pool_mixer(z_pool, pool_w[i], pool_scale[i])
        y_attn = chunk_attention(q, k, v, q_norm[i], k_norm[i], rel_bias[i])
        gates = jax.nn.sigmoid(u @ w_branch_gate[i] + b_branch_gate[i])
        g_pool = gates[..., :D_MODEL]
        g_attn = gates[..., D_MODEL:]
        merged = g_pool * (y_pool @ w_br_pool[i]) + g_attn * (y_attn @ w_br_attn[i])
        h = h + merged @ w_out[i]
        h = h + 0.5 * swiglu(rms_norm(h, ffn2_norm[i]), ffn2_w_gate[i], ffn2_w_up[i], ffn2_w_down[i])
        ple_gate = jax.nn.sigmoid(rms_norm(h, ple_norm[i]) @ w_ple_gate[i])
        h = h + ple_gate * (p[i] @ w_ple[i])
    return h
```

```python
import contextlib
import numpy as np
import concourse.bass as bass
import concourse.mybir as mybir
from concourse.bass_utils import run_bass_kernel_spmd

F32 = mybir.dt.float32
BF16 = mybir.dt.bfloat16
AF = mybir.ActivationFunctionType
ALU = mybir.AluOpType

NCORES = 8
D = 2048
DC = D // 128
SEQ = 16384
TOK = SEQ // NCORES
HALO = 512
DFF = 5632
FC = DFF // 128
NH = 8
EPS = 1e-6
NEG = -30000.0
POOL_WINDOWS = (2, 4, 8, 16)
NSLOT = 5
NTMP = 5
SLOT = 4096

C_FFN1 = 0
C_MIX = 16
C_FFN2 = 32
C_PLE = 48
C_BG = 64
C_PSC = 96
C_QN = 104
C_KN = 105
C_INVC = 106
NCONST = 176


class Res:
    __slots__ = ("name", "writer", "readers")

    def __init__(self, name):
        self.name = name
        self.writer = None
        self.readers = {}


class Op:
    __slots__ = ("eng", "fn", "deps", "need_sig", "sigval", "sem", "semval", "uid")


class Sched:
    ENGS = ("pe", "act", "dve", "pool", "sp")

    def __init__(self):
        self.ops = {e: [] for e in self.ENGS}
        self.uid = 0
        self.dma_counts = {}
        self.open_batches = {}

    def end_batch(self, sem):
        tot = self.dma_counts.get(id(sem), 0)
        for op in self.open_batches.pop(id(sem), []):
            op.semval = tot

    def add(self, eng, fn, reads=(), writes=(), dma_sem=None, batch=False):
        op = Op()
        op.eng = eng
        op.fn = fn
        op.need_sig = False
        op.sigval = None
        op.sem = None
        op.semval = None
        self.uid += 1
        op.uid = self.uid
        deps = {}
        for r in reads:
            if r.writer is not None:
                deps[r.writer.uid] = r.writer
        for w in writes:
            if w.writer is not None:
                deps[w.writer.uid] = w.writer
            for o in w.readers.values():
                deps[o.uid] = o
        deps.pop(op.uid, None)
        op.deps = list(deps.values())
        if dma_sem is not None:
            op.sem = dma_sem
            self.dma_counts[id(dma_sem)] = self.dma_counts.get(id(dma_sem), 0) + 16
            op.semval = self.dma_counts[id(dma_sem)]
            if batch:
                self.open_batches.setdefault(id(dma_sem), []).append(op)
        for r in reads:
            key = eng if op.sem is None else ("dma", op.uid)
            r.readers[key] = op
        for w in writes:
            w.writer = op
            w.readers = {}
        self.ops[eng].append(op)
        return op

    def emit(self, nc, block, prog):
        assert not self.open_batches, "unclosed DMA batch"
        for e in self.ENGS:
            for op in self.ops[e]:
                for d in op.deps:
                    if d.sem is None:
                        if d.eng == "pe" and op.eng == "pe" and op.sem is None:
                            continue
                        d.need_sig = True
        for e in self.ENGS:
            c = 0
            for op in self.ops[e]:
                if op.need_sig:
                    c += 1
                    op.sigval = c
        engobj = {"pe": "tensor", "act": "scalar", "dve": "vector", "pool": "gpsimd", "sp": "sync"}

        def make(e):
            ops = self.ops[e]

            def body(eng):
                waited = {}
                for op in ops:
                    for d in op.deps:
                        if d.sem is not None:
                            sem, val = d.sem, d.semval
                        else:
                            if d.eng == "pe" and op.eng == "pe" and op.sem is None:
                                continue
                            sem, val = prog[d.eng], d.sigval
                        if waited.get(id(sem), 0) >= val:
                            continue
                        waited[id(sem)] = val
                        eng.wait_ge(sem, val)
                    inst = op.fn(eng)
                    if op.sem is not None:
                        inst.then_inc(op.sem, 16)
                    elif op.need_sig:
                        inst.then_inc(prog[e], 1)
            return body

        for e in self.ENGS:
            if not self.ops[e]:
                continue
            getattr(block, engobj[e])(make(e))


def build_program(stages=("ffn1", "mix", "ffn2", "ple"), ntiles=2):
    nc = bass.Bass("TRN2", target_bir_lowering=False)
    dt_in = lambda name, shape: nc.dram_tensor(name, list(shape), F32, kind="ExternalInput").ap()
    xT = dt_in("xT", [D, HALO + TOK])
    pT = dt_in("pT", [256, TOK])
    consts_d = dt_in("consts", [128, NCONST])
    halo_neg_d = dt_in("halo_neg", [128, 128])
    biasT_d = dt_in("biasT", [128, NH * 5 * 128])
    ident_d = dt_in("ident", [128, 128])
    W = {}
    for pre in ("ffn1", "ffn2"):
        W[pre + "_w_gate"] = dt_in(pre + "_w_gate", [D, DFF])
        W[pre + "_w_up"] = dt_in(pre + "_w_up", [D, DFF])
        W[pre + "_w_down"] = dt_in(pre + "_w_down", [DFF, D])
    W["w_in"] = dt_in("w_in", [D, 4096])
    W["pool_w"] = dt_in("pool_w", [4, 256, 256])
    W["w_br_pool"] = dt_in("w_br_pool", [1024, D])
    W["w_br_attn"] = dt_in("w_br_attn", [1024, D])
    W["w_branch_gate"] = dt_in("w_branch_gate", [D, 2 * D])
    W["w_out"] = dt_in("w_out", [D, D])
    W["w_ple_gate"] = dt_in("w_ple_gate", [D, D])
    W["w_ple"] = dt_in("w_ple", [256, D])
    outT = nc.dram_tensor("outT", [D, TOK], F32, kind="ExternalOutput").ap()

    S = Sched()
    es = contextlib.ExitStack()
    with es:
        es.enter_context(nc.allow_low_precision("bf16 matmul operands, fp32 accumulation"))
        sb = lambda name, shape, dt: es.enter_context(nc.sbuf_tensor(name, list(shape), dt))
        H = sb("H", [128, DC, 1024], F32)
        XN = sb("XN", [128, DC, 1024], BF16)
        WS = sb("WS", [128, NSLOT, SLOT], BF16)
        KB = sb("KB", [128, NH, 1024], BF16)
        VB = sb("VB", [128, 8, 1024], BF16)
        YP = sb("YP", [128, 8, 512], BF16)
        YA = sb("YA", [128, 8, 512], BF16)
        HID = sb("HID", [128, 2, 2, 512], BF16)
        TMP = sb("TMP", [128, NTMP, 528], F32)
        ZT = sb("ZT", [128, 8, 16], F32)
        PT = sb("PT", [128, 2, 640], BF16)
        RD = sb("RD", [128, 2, 128], F32)
        SQ = sb("SQ", [128, 2, 512], BF16)
        CON = sb("CON", [128, NCONST], F32)
        EPSB = sb("EPSB", [128, 2], F32)
        ONES_D = sb("ONES_D", [128, 128], BF16)
        ONES_H = sb("ONES_H", [128, 128], BF16)
        ONES_1 = sb("ONES_1", [128, 128], BF16)
        IDENT = sb("IDENT", [128, 128], BF16)
        HNEG = sb("HNEG", [128, 128], BF16)
        PS = es.enter_context(nc.psum_tensor("PS", [128, 8, 512], F32))

        nsem = lambda name: es.enter_context(nc.semaphore(name))
        prog = {e: nsem("prog_" + e) for e in Sched.ENGS}
        slot_sem = [nsem("slot%d" % i) for i in range(NSLOT)]
        sem_in = nsem("sem_in")
        sem_xj = [nsem("sem_x%d" % j) for j in range(DC)]
        sem_stj = [nsem("sem_st%d" % j) for j in range(DC)]
        sem_p = nsem("sem_p")
        sem_in2 = nsem("sem_in2")

        r_H = [[Res("H%d_%d" % (j, s)) for s in range(2)] for j in range(DC)]
        r_XN = [[Res("XN%d_%d" % (j, s)) for s in range(2)] for j in range(DC)]
        r_slot = [Res("slot%d" % i) for i in range(NSLOT)]
        r_ps = [Res("ps%d" % i) for i in range(8)]
        r_hid = [Res("hid%d" % i) for i in range(2)]
        r_tmp = [Res("tmp%d" % i) for i in range(NTMP)]
        r_sq = [Res("sq%d" % i) for i in range(2)]
        r_con = Res("con")
        r_misc = Res("misc")
        r_KB = [[Res("KB%d_%d" % (h, s)) for s in range(2)] for h in range(NH)]
        r_VB = [[[Res("VB%d_%d_%d" % (b, s, p)) for p in range(4)] for s in range(2)] for b in range(4)]
        r_YP = [Res("YP%d" % i) for i in range(8)]
        r_YA = [Res("YA%d" % i) for i in range(8)]
        r_ZT = [Res("ZT%d" % i) for i in range(8)]
        r_pt = [Res("pt%d" % i) for i in range(2)]
        r_rd = [Res("rd%d" % i) for i in range(2)]

        cnt = {"ps": 0, "tmp": 0, "sq": 0, "hid": 0, "pt": 0, "rd": 0}

        def next_ps():
            b = cnt["ps"] % 8
            cnt["ps"] += 1
            return b

        def next_of(kind, n):
            b = cnt[kind] % n
            cnt[kind] += 1
            return b

        ws_state = {"n": 0}

        slot_owner = [None] * NSLOT

        class Blk:
            @property
            def ap(self):
                assert slot_owner[self.slot] is self, "stale weight block (ring too small)"
                return self._ap

        def wload(src_ap, dims):
            a, b = dims
            assert a * b <= SLOT
            i = ws_state["n"]
            ws_state["n"] += 1
            s = i % NSLOT
            view = WS[:, s, 0:a * b].rearrange("p (a b) -> p a b", a=a)
            S.add("pool", lambda eng, o=view, i_=src_ap: eng.dma_start(out=o, in_=i_),
                  writes=[r_slot[s]], dma_sem=slot_sem[s])
            blk = Blk()
            blk._ap = view
            blk.slot = s
            blk.res = r_slot[s]
            slot_owner[s] = blk
            return blk

        def wcols(name, c0, ncols):
            w = W[name]
            kc = w.shape[0] // 128
            src = w.rearrange("(kc p) n -> p kc n", p=128)[:, :, c0:c0 + ncols]
            return wload(src, (kc, ncols))

        def wrows(name, r0, nrows):
            w = W[name]
            src = w[r0:r0 + nrows, :].rearrange("(f p) n -> p f n", p=128)
            return wload(src, (nrows // 128, w.shape[1]))

        r_hneg = Res("hneg")
        S.add("sp", lambda eng: eng.dma_start(out=CON[:, :], in_=consts_d[:, :]),
              writes=[r_con], dma_sem=sem_in, batch=True)
        S.add("pool", lambda eng: eng.dma_start(out=HNEG[:, :], in_=halo_neg_d[:, :]),
              writes=[r_hneg], dma_sem=sem_in2, batch=True)
        r_ident = Res("ident")
        S.add("pool", lambda eng: eng.dma_start(out=IDENT[:, :], in_=ident_d[:, :]),
              writes=[r_ident], dma_sem=sem_in2, batch=True)
        S.end_batch(sem_in)
        S.end_batch(sem_in2)
        S.add("dve", lambda eng: eng.memset(ONES_D[:, :], 1.0 / D), writes=[r_misc])
        S.add("dve", lambda eng: eng.memset(ONES_H[:, :], 1.0 / 128), writes=[r_misc])
        S.add("dve", lambda eng: eng.memset(ONES_1[:, :], 1.0), writes=[r_misc])
        S.add("dve", lambda eng: eng.memset(EPSB[:, 0:1], EPS), writes=[r_misc])
        S.add("dve", lambda eng: eng.memset(EPSB[:, 1:2], EPS * 128.0), writes=[r_misc])

        def rstd_from_ps(b, t, epscol=0, ncol=512):
            S.add("act", lambda eng: eng.activation(
                out=TMP[:, t, 0:ncol], in_=PS[:, b, 0:ncol], func=AF.Sqrt,
                bias=EPSB[:, epscol:epscol + 1]),
                reads=[r_ps[b], r_misc], writes=[r_tmp[t]])
            S.add("dve", lambda eng: eng.reciprocal(out=TMP[:, t, 0:ncol], in_=TMP[:, t, 0:ncol]),
                  reads=[r_tmp[t]], writes=[r_tmp[t]])

        def rmsnorm_to_xn(gcol, subs, col_of_sub, xn_dst):
            for s in subs:
                c0 = s * 512
                b = next_ps()
                for j in range(DC):
                    q = next_of("sq", 2)
                    S.add("act", lambda eng, j=j, q=q, c0=c0: eng.activation(
                        out=SQ[:, q, :], in_=H[:, j, c0:c0 + 512], func=AF.Square),
                        reads=[r_H[j][s]], writes=[r_sq[q]])
                    S.add("pe", lambda eng, j=j, q=q, b=b: eng.matmul(
                        PS[:, b, :], lhsT=ONES_D[:, :], rhs=SQ[:, q, :],
                        start=(j == 0), stop=(j == DC - 1)),
                        reads=[r_sq[q], r_misc], writes=[r_ps[b]])
                t = next_of("tmp", NTMP)
                rstd_from_ps(b, t, 0)
                for j in range(DC):
                    dst, rdst = xn_dst(j, s)
                    S.add("dve", lambda eng, j=j, t=t, c0=c0, dst=dst: eng.scalar_tensor_tensor(
                        out=dst, in0=H[:, j, c0:c0 + 512], scalar=CON[:, gcol + j:gcol + j + 1],
                        in1=TMP[:, t, 0:512], op0=ALU.mult, op1=ALU.mult),
                        reads=[r_H[j][s], r_tmp[t], r_con], writes=[rdst])

        def xn_full(j, s):
            return XN[:, j, s * 512:(s + 1) * 512], r_XN[j][s]

        def ffn(pre, gcol, subs):
            rmsnorm_to_xn(gcol, subs[:1], None, xn_full)
            groups = list(range(FC // 2))
            items = [(g, s) for g in groups for s in subs]
            blks = {}

            def get_gu(g):
                if ("g", g) not in blks:
                    blks[("g", g)] = wcols(pre + "_w_gate", g * 256, 256)
                    blks[("u", g)] = wcols(pre + "_w_up", g * 256, 256)
                return blks[("g", g)], blks[("u", g)]

            def get_d(g):
                if ("d", g) not in blks:
                    blks[("d", g)] = wrows(pre + "_w_down", g * 256, 256)
                return blks[("d", g)]

            hid_of = {}

            def GU(g, s):
                bg, bu = get_gu(g)
                hb = next_of("hid", 2)
                hid_of[(g, s)] = hb
                for f in range(2):
                    pg = next_ps()
                    pu = next_ps()
                    for (blk, pb) in ((bg, pg), (bu, pu)):
                        for k in range(DC):
                            S.add("pe", lambda eng, pb=pb, k=k, s=s, lhsT=blk.ap[:, k, f * 128:(f + 1) * 128]: eng.matmul(
                                PS[:, pb, :], lhsT=lhsT,
                                rhs=XN[:, k, s * 512:(s + 1) * 512],
                                start=(k == 0), stop=(k == DC - 1)),
                                reads=[blk.res, r_XN[k][s]], writes=[r_ps[pb]])
                    t = next_of("tmp", NTMP)
                    S.add("act", lambda eng, pg=pg, t=t: eng.activation(
                        out=TMP[:, t, 0:512], in_=PS[:, pg, :], func=AF.Silu),
                        reads=[r_ps[pg]], writes=[r_tmp[t]])
                    S.add("dve", lambda eng, pu=pu, t=t, hb=hb, f=f: eng.tensor_tensor(
                        out=HID[:, hb, f, :], in0=PS[:, pu, :], in1=TMP[:, t, 0:512], op=ALU.mult),
                        reads=[r_ps[pu], r_tmp[t]], writes=[r_hid[hb]])

            def DN(g, s):
                bd = get_d(g)
                hb = hid_of[(g, s)]
                for j in range(DC):
                    pb = next_ps()
                    for f in range(2):
                        S.add("pe", lambda eng, pb=pb, f=f, hb=hb, lhsT=bd.ap[:, f, j * 128:(j + 1) * 128]: eng.matmul(
                            PS[:, pb, :], lhsT=lhsT,
                            rhs=HID[:, hb, f, :], start=(f == 0), stop=(f == 1)),
                            reads=[bd.res, r_hid[hb]], writes=[r_ps[pb]])
                    S.add("dve", lambda eng, pb=pb, j=j, s=s: eng.scalar_tensor_tensor(
                        out=H[:, j, s * 512:(s + 1) * 512], in0=PS[:, pb, :], scalar=0.5,
                        in1=H[:, j, s * 512:(s + 1) * 512], op0=ALU.mult, op1=ALU.add),
                        reads=[r_ps[pb], r_H[j][s]], writes=[r_H[j][s]])

            for idx, (g, s) in enumerate(items):
                GU(g, s)
                if idx == 0 and len(subs) > 1:
                    rmsnorm_to_xn(gcol, subs[1:], None, xn_full)
                if idx > 0:
                    DN(*items[idx - 1])
            DN(*items[-1])

        def load_x_chunk(j, col0, subs):
            n = len(subs)
            S.add("sp", lambda eng: eng.dma_start(
                out=H[:, j, 0:n * 512], in_=xT[j * 128:(j + 1) * 128, col0:col0 + n * 512]),
                writes=[r_H[j][s] for s in subs], dma_sem=sem_xj[j])

        def load_x(col0, subs):
            for j in range(DC):
                load_x_chunk(j, col0, subs)

        def store_chunk(tile, j):
            S.add("sp", lambda eng: eng.dma_start(
                out=outT[j * 128:(j + 1) * 128, tile * 1024:(tile + 1) * 1024], in_=H[:, j, :]),
                reads=[r_H[j][0], r_H[j][1]], dma_sem=sem_stj[j])

        def U_of(k):
            return XN[:, k, 0:512]

        def xn_u(j, s):
            return XN[:, j, 0:512], r_XN[j][0]

        def proj_fm(blk, col0, rhs_of, rres_of, nk, ncol=512, rcol0=0):
            b = next_ps()
            for k in range(nk):
                lhsT = blk.ap[:, k, col0:col0 + 128]
                rhs = rhs_of(k)
                S.add("pe", lambda eng, k=k, lhsT=lhsT, rhs=rhs: eng.matmul(
                    PS[:, b, 0:ncol], lhsT=lhsT, rhs=rhs,
                    start=(k == 0), stop=(k == nk - 1)),
                    reads=[blk.res, rres_of(k)], writes=[r_ps[b]])
            return b

        def mix(s, gs, halo_only=False, pre_normed=False, norm_next=None):
            cur = gs % 2
            prv = 1 - cur
            c0 = s * 512
            if not pre_normed:
                rmsnorm_to_xn(C_MIX, [s], None, xn_u)
            u_rhs = lambda k: U_of(k)
            u_res = lambda k: r_XN[k][0]

            pw = None
            for c in range(8):
                if c % 2 == 0:
                    zblk = wcols("w_in", (c // 2) * 256, 256)
                if c == 2 and not halo_only:
                    pw_src = W["pool_w"].rearrange("g (kc p) n -> p (g kc) n", p=128)
                    pw = wload(pw_src, (8, 256))
                g = c // 2
                w = POOL_WINDOWS[g]
                if halo_only:
                    b = next_ps()
                    for k in range(DC):
                        S.add("pe", lambda eng, k=k, b=b, lhsT=zblk.ap[:, k, (c % 2) * 128:(c % 2) * 128 + 128]: eng.matmul(
                            PS[:, b, 0:16], lhsT=lhsT,
                            rhs=XN[:, k, 496:512], start=(k == 0), stop=(k == DC - 1)),
                            reads=[zblk.res, r_XN[k][0]], writes=[r_ps[b]])
                    S.add("act", lambda eng, b=b, c=c: eng.activation(
                        out=ZT[:, c, :], in_=PS[:, b, 0:16], func=AF.Copy),
                        reads=[r_ps[b]], writes=[r_ZT[c]])
                    continue
                b = proj_fm(zblk, (c % 2) * 128, u_rhs, u_res, DC)
                tz = next_of("tmp", NTMP)
                S.add("dve", lambda eng, tz=tz, c=c: eng.tensor_copy(out=TMP[:, tz, 0:16], in_=ZT[:, c, :]),
                      reads=[r_ZT[c]], writes=[r_tmp[tz]])
                S.add("act", lambda eng, tz=tz, b=b: eng.activation(
                    out=TMP[:, tz, 16:528], in_=PS[:, b, :], func=AF.Copy),
                    reads=[r_ps[b]], writes=[r_tmp[tz]])
                S.add("dve", lambda eng, tz=tz, c=c: eng.tensor_copy(out=ZT[:, c, :], in_=TMP[:, tz, 512:528]),
                      reads=[r_tmp[tz]], writes=[r_ZT[c]])
                tc_ = tz
                n = 1
                while n < w:
                    tn = next_of("tmp", NTMP)
                    assert tn != tz
                    S.add("dve", lambda eng, tn=tn, tc_=tc_, n=n: eng.tensor_tensor(
                        out=TMP[:, tn, 2 * n - 1:528], in0=TMP[:, tc_, 2 * n - 1:528],
                        in1=TMP[:, tc_, n - 1:528 - n], op=ALU.add),
                        reads=[r_tmp[tc_]], writes=[r_tmp[tn]])
                    tc_ = tn
                    n *= 2
                S.add("dve", lambda eng, tc_=tc_, tz=tz, c=c, w=w: eng.scalar_tensor_tensor(
                    out=YA[:, c, :], in0=TMP[:, tc_, 16:528], scalar=1.0 / w, in1=TMP[:, tz, 16:528],
                    op0=ALU.mult, op1=ALU.subtract),
                    reads=[r_tmp[tc_], r_tmp[tz]], writes=[r_YA[c]])
                if gs == 1:
                    i = next_of("rd", 2)
                    S.add("dve", lambda eng, tc_=tc_, i=i, g=g: eng.tensor_tensor(
                        out=RD[:, i, 0:16], in0=TMP[:, tc_, 16:32],
                        in1=CON[:, C_INVC + g * 16:C_INVC + (g + 1) * 16], op=ALU.mult),
                        reads=[r_tmp[tc_], r_con], writes=[r_rd[i]])
                    S.add("dve", lambda eng, tz=tz, i=i, c=c: eng.tensor_tensor(
                        out=YA[:, c, 0:16], in0=RD[:, i, 0:16], in1=TMP[:, tz, 16:32], op=ALU.subtract),
                        reads=[r_rd[i], r_tmp[tz]], writes=[r_YA[c]])

                def group_mm(g):
                    for o in range(2):
                        b = next_ps()
                        for ci in range(2):
                            cc = 2 * g + ci
                            S.add("pe", lambda eng, b=b, ci=ci, cc=cc, lhsT=pw.ap[:, g * 2 + ci, o * 128:(o + 1) * 128]: eng.matmul(
                                PS[:, b, :], lhsT=lhsT,
                                rhs=YA[:, cc, :], start=(ci == 0), stop=(ci == 1)),
                                reads=[pw.res, r_YA[cc]], writes=[r_ps[b]])
                        idx = 2 * g + o
                        S.add("dve", lambda eng, b=b, idx=idx: eng.tensor_scalar(
                            out=YP[:, idx, :], in0=PS[:, b, :], scalar1=CON[:, C_PSC + idx:C_PSC + idx + 1],
                            scalar2=None, op0=ALU.mult),
                            reads=[r_ps[b], r_con], writes=[r_YP[idx]])
                if c % 2 == 1:
                    if g > 0:
                        group_mm(g - 1)
                    if g == 3:
                        group_mm(3)

            bias_blk = {}
            for pr in range(4):
                if not halo_only:
                    qblk = wcols("w_in", 1024 + pr * 256, 256)
                kblk = wcols("w_in", 2048 + pr * 256, 256)
                vblk = wcols("w_in", 3072 + pr * 256, 256)
                if not halo_only:
                    bsrc = biasT_d[:, pr * 1280:(pr + 1) * 1280].rearrange("p (a b) -> p a b", a=2)
                    bias_cur = wload(bsrc, (2, 640))
                pend = []
                if not halo_only:
                    for hh in range(2):
                        pend.append(("q", hh, proj_fm(qblk, hh * 128, u_rhs, u_res, DC)))
                for hh in range(2):
                    pend.append(("k", hh, proj_fm(kblk, hh * 128, u_rhs, u_res, DC)))
                for (kind, hh, b) in pend:
                    hd = 2 * pr + hh
                    q = next_of("sq", 2)
                    S.add("act", lambda eng, b=b, q=q: eng.activation(
                        out=SQ[:, q, :], in_=PS[:, b, :], func=AF.Square),
                        reads=[r_ps[b]], writes=[r_sq[q]])
                    b2 = next_ps()
                    ones = ONES_1 if kind == "q" else ONES_H
                    S.add("pe", lambda eng, b2=b2, q=q, ones=ones: eng.matmul(
                        PS[:, b2, :], lhsT=ones[:, :], rhs=SQ[:, q, :], start=True, stop=True),
                        reads=[r_sq[q], r_misc], writes=[r_ps[b2]])
                    t = next_of("tmp", NTMP)
                    rstd_from_ps(b2, t, 1 if kind == "q" else 0)
                    if kind == "q":
                        dst, rdst, gcol = HID[:, hh, 0, :], r_hid[hh], C_QN
                    else:
                        dst, rdst, gcol = KB[:, hd, cur * 512:(cur + 1) * 512], r_KB[hd][cur], C_KN
                    S.add("dve", lambda eng, b=b, t=t, dst=dst, gcol=gcol: eng.scalar_tensor_tensor(
                        out=dst, in0=PS[:, b, :], scalar=CON[:, gcol:gcol + 1], in1=TMP[:, t, 0:512],
                        op0=ALU.mult, op1=ALU.mult),
                        reads=[r_ps[b], r_tmp[t], r_con], writes=[rdst])
                for tb2 in range(2):
                    b = next_ps()
                    for half in range(2):
                        tb = tb2 * 2 + half
                        for k in range(DC):
                            S.add("pe", lambda eng, b=b, half=half, tb=tb, k=k, rhs=vblk.ap[:, k, :]: eng.matmul(
                                PS[:, b, half * 256:(half + 1) * 256],
                                lhsT=XN[:, k, tb * 128:(tb + 1) * 128], rhs=rhs,
                                start=(k == 0), stop=(k == DC - 1)),
                                reads=[vblk.res, r_XN[k][0]], writes=[r_ps[b]])
                    for half in range(2):
                        tb = tb2 * 2 + half
                        S.add("act", lambda eng, b=b, half=half, tb=tb, pr=pr: eng.activation(
                            out=VB[:, cur * 4 + tb, pr * 256:(pr + 1) * 256],
                            in_=PS[:, b, half * 256:(half + 1) * 256], func=AF.Copy),
                            reads=[r_ps[b]], writes=[r_VB[tb][cur][pr]])
                if halo_only:
                    continue
                def att_scores(hh, j):
                    hd = 2 * pr + hh
                    bA = next_ps()
                    bB = next_ps()
                    for r in range(5):
                        L = j + r
                        hf = prv if L < 4 else cur
                        kcol = hf * 512 + (L % 4) * 128
                        pb, pc = (bA, r * 128) if r < 4 else (bB, 0)
                        need_mask = (gs == 1 and L < 4)
                        S.add("pe", lambda eng, pb=pb, pc=pc, hd=hd, hh=hh, kcol=kcol, j=j: eng.matmul(
                            PS[:, pb, pc:pc + 128], lhsT=KB[:, hd, kcol:kcol + 128],
                            rhs=HID[:, hh, 0, j * 128:(j + 1) * 128], start=True, stop=False),
                            reads=[r_KB[hd][hf], r_hid[hh]], writes=[r_ps[pb]])
                        S.add("pe", lambda eng, pb=pb, pc=pc, need_mask=need_mask, rhs=bias_cur.ap[:, hh, r * 128:(r + 1) * 128]: eng.matmul(
                            PS[:, pb, pc:pc + 128], lhsT=IDENT[:, :],
                            rhs=rhs, start=False, stop=(not need_mask)),
                            reads=[bias_cur.res, r_ident], writes=[r_ps[pb]])
                        if need_mask:
                            S.add("pe", lambda eng, pb=pb, pc=pc: eng.matmul(
                                PS[:, pb, pc:pc + 128], lhsT=IDENT[:, :], rhs=HNEG[:, :],
                                start=False, stop=True),
                                reads=[r_hneg, r_ident], writes=[r_ps[pb]])
                    pi = next_of("pt", 2)
                    S.add("act", lambda eng, bA=bA, pi=pi: eng.activation(
                        out=PT[:, pi, 0:512], in_=PS[:, bA, :], func=AF.Exp),
                        reads=[r_ps[bA]], writes=[r_pt[pi]])
                    S.add("act", lambda eng, bB=bB, pi=pi: eng.activation(
                        out=PT[:, pi, 512:640], in_=PS[:, bB, 0:128], func=AF.Exp),
                        reads=[r_ps[bB]], writes=[r_pt[pi]])
                    return pi

                def att_pv(hh, j, pi):
                    hd = 2 * pr + hh
                    bO = next_ps()
                    for r in range(5):
                        L = j + r
                        hf = prv if L < 4 else cur
                        S.add("pe", lambda eng, bO=bO, r=r, L=L, hf=hf, hd=hd, pi=pi: eng.matmul(
                            PS[:, bO, 0:128], lhsT=VB[:, hf * 4 + (L % 4), hd * 128:(hd + 1) * 128],
                            rhs=PT[:, pi, r * 128:(r + 1) * 128], start=(r == 0), stop=(r == 4)),
                            reads=[r_VB[L % 4][hf][pr], r_pt[pi]], writes=[r_ps[bO]])
                    for r in range(5):
                        S.add("pe", lambda eng, bO=bO, r=r, pi=pi: eng.matmul(
                            PS[:, bO, 128:256], lhsT=ONES_1[:, :],
                            rhs=PT[:, pi, r * 128:(r + 1) * 128], start=(r == 0), stop=(r == 4)),
                            reads=[r_pt[pi], r_misc], writes=[r_ps[bO]])
                    i = next_of("rd", 2)
                    S.add("dve", lambda eng, bO=bO, i=i: eng.reciprocal(out=RD[:, i, :], in_=PS[:, bO, 128:256]),
                          reads=[r_ps[bO]], writes=[r_rd[i]])
                    S.add("dve", lambda eng, bO=bO, i=i, hd=hd, j=j: eng.tensor_tensor(
                        out=YA[:, hd, j * 128:(j + 1) * 128], in0=PS[:, bO, 0:128], in1=RD[:, i, :],
                        op=ALU.mult),
                        reads=[r_ps[bO], r_rd[i]], writes=[r_YA[hd]])

                aitems = [(hh, j) for hh in range(2) for j in range(4)]
                prev_it = None
                for it in aitems:
                    pi = att_scores(*it)
                    if prev_it is not None:
                        att_pv(*prev_it)
                    prev_it = (it[0], it[1], pi)
                att_pv(*prev_it)
            if halo_only:
                return

            for jp in range(8):
                tt = {}
                wga = wcols("w_branch_gate", jp * 256, 256)
                for o in range(2):
                    j = 2 * jp + o
                    bG = proj_fm(wga, o * 128, u_rhs, u_res, DC)
                    t1 = next_of("tmp", NTMP)
                    tt[("a", o)] = t1
                    S.add("act", lambda eng, bG=bG, t1=t1, j=j: eng.activation(
                        out=TMP[:, t1, 0:512], in_=PS[:, bG, :], func=AF.Sigmoid,
                        bias=CON[:, C_BG + j:C_BG + j + 1]),
                        reads=[r_ps[bG], r_con], writes=[r_tmp[t1]])
                wgb = wcols("w_branch_gate", D + jp * 256, 256)
                for o in range(2):
                    j = 2 * jp + o
                    bG = proj_fm(wgb, o * 128, u_rhs, u_res, DC)
                    t2 = next_of("tmp", NTMP)
                    tt[("b", o)] = t2
                    S.add("act", lambda eng, bG=bG, t2=t2, j=j: eng.activation(
                        out=TMP[:, t2, 0:512], in_=PS[:, bG, :], func=AF.Sigmoid,
                        bias=CON[:, C_BG + 16 + j:C_BG + 16 + j + 1]),
                        reads=[r_ps[bG], r_con], writes=[r_tmp[t2]])
                wa = wcols("w_br_pool", jp * 256, 256)
                for o in range(2):
                    bY = proj_fm(wa, o * 128, lambda k: YP[:, k, :], lambda k: r_YP[k], 8)
                    t1 = tt[("a", o)]
                    S.add("dve", lambda eng, bY=bY, t1=t1: eng.tensor_tensor(
                        out=TMP[:, t1, 0:512], in0=PS[:, bY, :], in1=TMP[:, t1, 0:512], op=ALU.mult),
                        reads=[r_ps[bY], r_tmp[t1]], writes=[r_tmp[t1]])
                wb = wcols("w_br_attn", jp * 256, 256)
                for o in range(2):
                    j = 2 * jp + o
                    bY = proj_fm(wb, o * 128, lambda k: YA[:, k, :], lambda k: r_YA[k], 8)
                    t1 = tt[("a", o)]
                    t2 = tt[("b", o)]
                    S.add("dve", lambda eng, bY=bY, t2=t2: eng.tensor_tensor(
                        out=TMP[:, t2, 0:512], in0=PS[:, bY, :], in1=TMP[:, t2, 0:512], op=ALU.mult),
                        reads=[r_ps[bY], r_tmp[t2]], writes=[r_tmp[t2]])
                    S.add("dve", lambda eng, t1=t1, t2=t2, j=j: eng.tensor_tensor(
                        out=XN[:, j, 512:1024], in0=TMP[:, t1, 0:512], in1=TMP[:, t2, 0:512], op=ALU.add),
                        reads=[r_tmp[t1], r_tmp[t2]], writes=[r_XN[j][1]])

            if norm_next is not None:
                rmsnorm_to_xn(C_MIX, [norm_next], None, xn_u)

            for ip in range(8):
                wo = wcols("w_out", ip * 256, 256)
                for o in range(2):
                    i = 2 * ip + o
                    b = proj_fm(wo, o * 128, lambda k: XN[:, k, 512:1024], lambda k: r_XN[k][1], DC)
                    S.add("dve", lambda eng, b=b, i=i: eng.tensor_tensor(
                        out=H[:, i, c0:c0 + 512], in0=PS[:, b, :], in1=H[:, i, c0:c0 + 512], op=ALU.add),
                        reads=[r_ps[b], r_H[i][s]], writes=[r_H[i][s]])

        def ple(tile):
            subs = [0, 1]
            rmsnorm_to_xn(C_PLE, subs, None, xn_full)
            psrc = pT[:, tile * 1024:(tile + 1) * 1024].rearrange("(kc p) n -> p kc n", p=128)
            pview = YP[:, 0:4, :].rearrange("p a b -> p (a b)").rearrange("p (a b) -> p a b", a=2)
            wview = YA[:, :, :].rearrange("p a b -> p (a b)").rearrange("p (a b) -> p a b", a=2)
            wsrc = W["w_ple"].rearrange("(f p) n -> p f n", p=128)
            S.add("pool", lambda eng: eng.dma_start(out=pview, in_=psrc),
                  writes=r_YP, dma_sem=sem_p, batch=True)
            S.add("pool", lambda eng: eng.dma_start(out=wview, in_=wsrc),
                  writes=r_YA, dma_sem=sem_p, batch=True)
            S.end_batch(sem_p)
            for jp in range(8):
                wpg = wcols("w_ple_gate", jp * 256, 256)
                for o in range(2):
                    j = 2 * jp + o
                    for s in subs:
                        bG = proj_fm(wpg, o * 128, lambda k: XN[:, k, s * 512:(s + 1) * 512],
                                     lambda k: r_XN[k][s], DC)
                        bE = next_ps()
                        for kc in range(2):
                            S.add("pe", lambda eng, bE=bE, kc=kc, j=j, s=s: eng.matmul(
                                PS[:, bE, :], lhsT=wview[:, kc, j * 128:(j + 1) * 128],
                                rhs=pview[:, kc, s * 512:(s + 1) * 512], start=(kc == 0), stop=(kc == 1)),
                                reads=r_YP + r_YA, writes=[r_ps[bE]])
                        t = next_of("tmp", NTMP)
                        S.add("act", lambda eng, bG=bG, t=t: eng.activation(
                            out=TMP[:, t, 0:512], in_=PS[:, bG, :], func=AF.Sigmoid),
                            reads=[r_ps[bG]], writes=[r_tmp[t]])
                        S.add("dve", lambda eng, bE=bE, t=t: eng.tensor_tensor(
                            out=TMP[:, t, 0:512], in0=PS[:, bE, :], in1=TMP[:, t, 0:512], op=ALU.mult),
                            reads=[r_ps[bE], r_tmp[t]], writes=[r_tmp[t]])
                        S.add("dve", lambda eng, t=t, j=j, s=s: eng.tensor_tensor(
                            out=H[:, j, s * 512:(s + 1) * 512], in0=H[:, j, s * 512:(s + 1) * 512],
                            in1=TMP[:, t, 0:512], op=ALU.add),
                            reads=[r_tmp[t], r_H[j][s]], writes=[r_H[j][s]])
                    store_chunk(tile, j)
                    if tile + 1 < ntiles:
                        load_x_chunk(j, HALO + (tile + 1) * 1024, [0, 1])

        gs = 0
        if "mix" in stages:
            load_x(0, [0])
            if "ffn1" in stages:
                ffn("ffn1", C_FFN1, [0])
            mix(0, 0, halo_only=True)
        for tile in range(ntiles):
            if tile == 0 or "ple" not in stages:
                load_x(HALO + tile * 1024, [0, 1])
            if "ffn1" in stages:
                ffn("ffn1", C_FFN1, [0, 1])
            if "mix" in stages:
                for s in range(2):
                    gs += 1
                    mix(s, gs, pre_normed=(s == 1), norm_next=(1 if s == 0 else None))
            if "ffn2" in stages:
                ffn("ffn2", C_FFN2, [0, 1])
            if "ple" in stages:
                ple(tile)
            else:
                for j in range(DC):
                    store_chunk(tile, j)
        for j in range(DC):
            S.add("sp", lambda eng, j=j, v=S.dma_counts.get(id(sem_stj[j]), 0): eng.wait_ge(sem_stj[j], v))

        block = es.enter_context(nc.Block())
        S.emit(nc, block, prog)
    return nc


_W_NAMES = ["ffn1_w_gate", "ffn1_w_up", "ffn1_w_down", "ffn2_w_gate", "ffn2_w_up", "ffn2_w_down",
            "w_in", "pool_w", "w_br_pool", "w_br_attn", "w_branch_gate", "w_out", "w_ple_gate", "w_ple"]


def make_in_maps(inputs):
    f = lambda a: np.ascontiguousarray(np.asarray(a, dtype=np.float32))
    x = f(inputs["x"])[0]
    p = f(inputs["p"])[0, 0]
    xT_full = np.ascontiguousarray(x.T)
    pT_full = np.ascontiguousarray(p.T)
    shared = {n: f(inputs[n])[0] for n in _W_NAMES}
    con = np.zeros((128, NCONST), np.float32)
    lay = lambda v: f(v).reshape(-1, 128).T
    con[:, C_FFN1:C_FFN1 + 16] = lay(inputs["ffn1_norm"][0])
    con[:, C_MIX:C_MIX + 16] = lay(inputs["mix_norm"][0])
    con[:, C_FFN2:C_FFN2 + 16] = lay(inputs["ffn2_norm"][0])
    con[:, C_PLE:C_PLE + 16] = lay(inputs["ple_norm"][0])
    con[:, C_BG:C_BG + 32] = lay(inputs["b_branch_gate"][0])
    con[:, C_PSC:C_PSC + 8] = lay(inputs["pool_scale"][0])
    con[:, C_QN] = f(inputs["q_norm"][0])
    con[:, C_KN] = f(inputs["k_norm"][0])
    rb = f(inputs["rel_bias"][0])
    kj = np.arange(640)[:, None]
    qi = np.arange(128)[None, :]
    dist = qi - kj + 512
    idx = np.clip(dist, -256, 256) + 256
    g = rb[:, idx]
    valid = np.where(qi < 64, kj < 576, kj >= 64)
    g = np.where(valid[None], g, np.float32(NEG)).astype(np.float32)
    biasT = np.ascontiguousarray(g.reshape(NH, 5, 128, 128).transpose(2, 0, 1, 3)).reshape(128, NH * 5 * 128)
    in_maps = []
    for c in range(NCORES):
        m = dict(shared)
        xs = np.zeros((D, HALO + TOK), np.float32)
        lo = c * TOK - HALO
        if lo >= 0:
            xs[:, :] = xT_full[:, lo:lo + HALO + TOK]
        else:
            xs[:, HALO:] = xT_full[:, 0:TOK]
        m["xT"] = xs
        m["pT"] = np.ascontiguousarray(pT_full[:, c * TOK:(c + 1) * TOK])
        cc = con.copy()
        for gi, w in enumerate(POOL_WINDOWS):
            t = np.arange(16) + c * TOK
            cc[:, C_INVC + gi * 16:C_INVC + (gi + 1) * 16] = (1.0 / np.minimum(t + 1, w)).astype(np.float32)[None, :]
        m["consts"] = cc
        m["halo_neg"] = np.full((128, 128), NEG if c == 0 else 0.0, np.float32)
        m["biasT"] = biasT
        m["ident"] = np.eye(128, dtype=np.float32)
        in_maps.append(m)
    return in_maps


_NC_CACHE = {}


def kernel(**inputs):
    in_maps = make_in_maps(inputs)
    key = "full"
    if key not in _NC_CACHE:
        _NC_CACHE[key] = build_program()
    nc = _NC_CACHE[key]
    res = run_bass_kernel_spmd(nc, in_maps, core_ids=list(range(NCORES)))
    outs = [np.asarray(r["outT"]) for r in res.results]
    out = np.concatenate([o.T for o in outs], axis=0)
    return np.ascontiguousarray(out.astype(np.float32))[None]
```

```python
import contextlib
import numpy as np
import concourse.bass as bass
import concourse.mybir as mybir
from concourse.bass_utils import run_bass_kernel_spmd

F32 = mybir.dt.float32
BF16 = mybir.dt.bfloat16
AF = mybir.ActivationFunctionType
ALU = mybir.AluOpType

NCORES = 8
D = 2048
DC = D // 128
SEQ = 16384
TOK = SEQ // NCORES
HALO = 512
DFF = 5632
FC = DFF // 128
NH = 8
EPS = 1e-6
NEG = -30000.0
POOL_WINDOWS = (2, 4, 8, 16)
NSLOT = 5
NTMP = 5
SLOT = 4096

C_FFN1 = 0
C_MIX = 16
C_FFN2 = 32
C_PLE = 48
C_BG = 64
C_PSC = 96
C_QN = 104
C_KN = 105
C_INVC = 106
NCONST = 176


class Res:
    __slots__ = ("name", "writer", "readers")

    def __init__(self, name):
        self.name = name
        self.writer = None
        self.readers = {}


class Op:
    __slots__ = ("eng", "fn", "deps", "need_sig", "sigval", "sem", "semval", "uid")


class Sched:
    ENGS = ("pe", "act", "dve", "pool", "sp")

    def __init__(self):
        self.ops = {e: [] for e in self.ENGS}
        self.uid = 0
        self.dma_counts = {}
        self.open_batches = {}

    def end_batch(self, sem):
        tot = self.dma_counts.get(id(sem), 0)
        for op in self.open_batches.pop(id(sem), []):
            op.semval = tot

    def add(self, eng, fn, reads=(), writes=(), dma_sem=None, batch=False):
        op = Op()
        op.eng = eng
        op.fn = fn
        op.need_sig = False
        op.sigval = None
        op.sem = None
        op.semval = None
        self.uid += 1
        op.uid = self.uid
        deps = {}
        for r in reads:
            if r.writer is not None:
                deps[r.writer.uid] = r.writer
        for w in writes:
            if w.writer is not None:
                deps[w.writer.uid] = w.writer
            for o in w.readers.values():
                deps[o.uid] = o
        deps.pop(op.uid, None)
        op.deps = list(deps.values())
        if dma_sem is not None:
            op.sem = dma_sem
            self.dma_counts[id(dma_sem)] = self.dma_counts.get(id(dma_sem), 0) + 16
            op.semval = self.dma_counts[id(dma_sem)]
            if batch:
                self.open_batches.setdefault(id(dma_sem), []).append(op)
        for r in reads:
            key = eng if op.sem is None else ("dma", op.uid)
            r.readers[key] = op
        for w in writes:
            w.writer = op
            w.readers = {}
        self.ops[eng].append(op)
        return op

    def emit(self, nc, block, prog):
        assert not self.open_batches, "unclosed DMA batch"
        for e in self.ENGS:
            for op in self.ops[e]:
                for d in op.deps:
                    if d.sem is None:
                        if d.eng == "pe" and op.eng == "pe" and op.sem is None:
                            continue
                        d.need_sig = True
        for e in self.ENGS:
            c = 0
            for op in self.ops[e]:
                if op.need_sig:
                    c += 1
                    op.sigval = c
        engobj = {"pe": "tensor", "act": "scalar", "dve": "vector", "pool": "gpsimd", "sp": "sync"}

        def make(e):
            ops = self.ops[e]

            def body(eng):
                waited = {}
                for op in ops:
                    for d in op.deps:
                        if d.sem is not None:
                            sem, val = d.sem, d.semval
                        else:
                            if d.eng == "pe" and op.eng == "pe" and op.sem is None:
                                continue
                            sem, val = prog[d.eng], d.sigval
                        if waited.get(id(sem), 0) >= val:
                            continue
                        waited[id(sem)] = val
                        eng.wait_ge(sem, val)
                    inst = op.fn(eng)
                    if op.sem is not None:
                        inst.then_inc(op.sem, 16)
                    elif op.need_sig:
                        inst.then_inc(prog[e], 1)
            return body

        for e in self.ENGS:
            if not self.ops[e]:
                continue
            getattr(block, engobj[e])(make(e))


def build_program(stages=("ffn1", "mix", "ffn2", "ple"), ntiles=2):
    nc = bass.Bass("TRN2", target_bir_lowering=False)
    dt_in = lambda name, shape: nc.dram_tensor(name, list(shape), F32, kind="ExternalInput").ap()
    xT = dt_in("xT", [D, HALO + TOK])
    pT = dt_in("pT", [256, TOK])
    consts_d = dt_in("consts", [128, NCONST])
    halo_neg_d = dt_in("halo_neg", [128, 128])
    biasT_d = dt_in("biasT", [128, NH * 5 * 128])
    ident_d = dt_in("ident", [128, 128])
    W = {}
    for pre in ("ffn1", "ffn2"):
        W[pre + "_w_gate"] = dt_in(pre + "_w_gate", [D, DFF])
        W[pre + "_w_up"] = dt_in(pre + "_w_up", [D, DFF])
        W[pre + "_w_down"] = dt_in(pre + "_w_down", [DFF, D])
    W["w_in"] = dt_in("w_in", [D, 4096])
    W["pool_w"] = dt_in("pool_w", [4, 256, 256])
    W["w_br_pool"] = dt_in("w_br_pool", [1024, D])
    W["w_br_attn"] = dt_in("w_br_attn", [1024, D])
    W["w_branch_gate"] = dt_in("w_branch_gate", [D, 2 * D])
    W["w_out"] = dt_in("w_out", [D, D])
    W["w_ple_gate"] = dt_in("w_ple_gate", [D, D])
    W["w_ple"] = dt_in("w_ple", [256, D])
    outT = nc.dram_tensor("outT", [D, TOK], F32, kind="ExternalOutput").ap()

    S = Sched()
    es = contextlib.ExitStack()
    with es:
        es.enter_context(nc.allow_low_precision("bf16 matmul operands, fp32 accumulation"))
        sb = lambda name, shape, dt: es.enter_context(nc.sbuf_tensor(name, list(shape), dt))
        H = sb("H", [128, DC, 1024], F32)
        XN = sb("XN", [128, DC, 1024], BF16)
        WS = sb("WS", [128, NSLOT, SLOT], BF16)
        KB = sb("KB", [128, NH, 1024], BF16)
        VB = sb("VB", [128, 8, 1024], BF16)
        YP = sb("YP", [128, 8, 512], BF16)
        YA = sb("YA", [128, 8, 512], BF16)
        HID = sb("HID", [128, 2, 2, 512], BF16)
        TMP = sb("TMP", [128, NTMP, 528], F32)
        ZT = sb("ZT", [128, 8, 16], F32)
        PT = sb("PT", [128, 3, 640], BF16)
        RD = sb("RD", [128, 2, 128], F32)
        SQ = sb("SQ", [128, 2, 512], BF16)
        CON = sb("CON", [128, NCONST], F32)
        EPSB = sb("EPSB", [128, 2], F32)
        ONES_D = sb("ONES_D", [128, 128], BF16)
        ONES_H = sb("ONES_H", [128, 128], BF16)
        ONES_1 = sb("ONES_1", [128, 128], BF16)
        IDENT = sb("IDENT", [128, 128], BF16)
        HNEG = sb("HNEG", [128, 128], BF16)
        PS = es.enter_context(nc.psum_tensor("PS", [128, 8, 512], F32))

        nsem = lambda name: es.enter_context(nc.semaphore(name))
        prog = {e: nsem("prog_" + e) for e in Sched.ENGS}
        slot_sem = [nsem("slot%d" % i) for i in range(NSLOT)]
        sem_in = nsem("sem_in")
        sem_xj = [nsem("sem_x%d" % j) for j in range(DC)]
        sem_stj = [nsem("sem_st%d" % j) for j in range(DC)]
        sem_p = nsem("sem_p")
        sem_in2 = nsem("sem_in2")

        r_H = [[Res("H%d_%d" % (j, s)) for s in range(2)] for j in range(DC)]
        r_XN = [[Res("XN%d_%d" % (j, s)) for s in range(2)] for j in range(DC)]
        r_slot = [Res("slot%d" % i) for i in range(NSLOT)]
        r_ps = [Res("ps%d" % i) for i in range(8)]
        r_hid = [Res("hid%d" % i) for i in range(2)]
        r_tmp = [Res("tmp%d" % i) for i in range(NTMP)]
        r_sq = [Res("sq%d" % i) for i in range(2)]
        r_con = Res("con")
        r_misc = Res("misc")
        r_KB = [[Res("KB%d_%d" % (h, s)) for s in range(2)] for h in range(NH)]
        r_VB = [[[Res("VB%d_%d_%d" % (b, s, p)) for p in range(4)] for s in range(2)] for b in range(4)]
        r_YP = [Res("YP%d" % i) for i in range(8)]
        r_YA = [Res("YA%d" % i) for i in range(8)]
        r_ZT = [Res("ZT%d" % i) for i in range(8)]
        r_pt = [Res("pt%d" % i) for i in range(3)]
        r_rd = [Res("rd%d" % i) for i in range(2)]

        cnt = {"ps": 0, "tmp": 0, "sq": 0, "hid": 0, "pt": 0, "rd": 0}

        def next_ps():
            b = cnt["ps"] % 8
            cnt["ps"] += 1
            return b

        def next_of(kind, n):
            b = cnt[kind] % n
            cnt[kind] += 1
            return b

        ws_state = {"n": 0}

        slot_owner = [None] * NSLOT

        class Blk:
            @property
            def ap(self):
                assert slot_owner[self.slot] is self, "stale weight block (ring too small)"
                return self._ap

        def wload(src_ap, dims):
            a, b = dims
            assert a * b <= SLOT
            i = ws_state["n"]
            ws_state["n"] += 1
            s = i % NSLOT
            view = WS[:, s, 0:a * b].rearrange("p (a b) -> p a b", a=a)
            S.add("pool", lambda eng, o=view, i_=src_ap: eng.dma_start(out=o, in_=i_),
                  writes=[r_slot[s]], dma_sem=slot_sem[s])
            blk = Blk()
            blk._ap = view
            blk.slot = s
            blk.res = r_slot[s]
            slot_owner[s] = blk
            return blk

        def wcols(name, c0, ncols):
            w = W[name]
            kc = w.shape[0] // 128
            src = w.rearrange("(kc p) n -> p kc n", p=128)[:, :, c0:c0 + ncols]
            return wload(src, (kc, ncols))

        def wrows(name, r0, nrows):
            w = W[name]
            src = w[r0:r0 + nrows, :].rearrange("(f p) n -> p f n", p=128)
            return wload(src, (nrows // 128, w.shape[1]))

        r_hneg = Res("hneg")
        S.add("sp", lambda eng: eng.dma_start(out=CON[:, :], in_=consts_d[:, :]),
              writes=[r_con], dma_sem=sem_in, batch=True)
        S.add("pool", lambda eng: eng.dma_start(out=HNEG[:, :], in_=halo_neg_d[:, :]),
              writes=[r_hneg], dma_sem=sem_in2, batch=True)
        r_ident = Res("ident")
        S.add("pool", lambda eng: eng.dma_start(out=IDENT[:, :], in_=ident_d[:, :]),
              writes=[r_ident], dma_sem=sem_in2, batch=True)
        S.end_batch(sem_in)
        S.end_batch(sem_in2)
        S.add("dve", lambda eng: eng.memset(ONES_D[:, :], 1.0 / D), writes=[r_misc])
        S.add("dve", lambda eng: eng.memset(ONES_H[:, :], 1.0 / 128), writes=[r_misc])
        S.add("dve", lambda eng: eng.memset(ONES_1[:, :], 1.0), writes=[r_misc])
        S.add("dve", lambda eng: eng.memset(EPSB[:, 0:1], EPS), writes=[r_misc])
        S.add("dve", lambda eng: eng.memset(EPSB[:, 1:2], EPS * 128.0), writes=[r_misc])

        def rstd_from_ps(b, t, epscol=0, ncol=512):
            S.add("act", lambda eng: eng.activation(
                out=TMP[:, t, 0:ncol], in_=PS[:, b, 0:ncol], func=AF.Sqrt,
                bias=EPSB[:, epscol:epscol + 1]),
                reads=[r_ps[b], r_misc], writes=[r_tmp[t]])
            S.add("dve", lambda eng: eng.reciprocal(out=TMP[:, t, 0:ncol], in_=TMP[:, t, 0:ncol]),
                  reads=[r_tmp[t]], writes=[r_tmp[t]])

        def rmsnorm_to_xn(gcol, subs, col_of_sub, xn_dst):
            for s in subs:
                c0 = s * 512
                b = next_ps()
                for j in range(DC):
                    q = next_of("sq", 2)
                    S.add("act", lambda eng, j=j, q=q, c0=c0: eng.activation(
                        out=SQ[:, q, :], in_=H[:, j, c0:c0 + 512], func=AF.Square),
                        reads=[r_H[j][s]], writes=[r_sq[q]])
                    S.add("pe", lambda eng, j=j, q=q, b=b: eng.matmul(
                        PS[:, b, :], lhsT=ONES_D[:, :], rhs=SQ[:, q, :],
                        start=(j == 0), stop=(j == DC - 1)),
                        reads=[r_sq[q], r_misc], writes=[r_ps[b]])
                t = next_of("tmp", NTMP)
                rstd_from_ps(b, t, 0)
                for j in range(DC):
                    dst, rdst = xn_dst(j, s)
                    S.add("dve", lambda eng, j=j, t=t, c0=c0, dst=dst: eng.scalar_tensor_tensor(
                        out=dst, in0=H[:, j, c0:c0 + 512], scalar=CON[:, gcol + j:gcol + j + 1],
                        in1=TMP[:, t, 0:512], op0=ALU.mult, op1=ALU.mult),
                        reads=[r_H[j][s], r_tmp[t], r_con], writes=[rdst])

        def xn_full(j, s):
            return XN[:, j, s * 512:(s + 1) * 512], r_XN[j][s]

        def ffn(pre, gcol, subs):
            rmsnorm_to_xn(gcol, subs[:1], None, xn_full)
            groups = list(range(FC // 2))
            items = [(g, s) for g in groups for s in subs]
            blks = {}

            def get_gu(g):
                if ("g", g) not in blks:
                    blks[("g", g)] = wcols(pre + "_w_gate", g * 256, 256)
                    blks[("u", g)] = wcols(pre + "_w_up", g * 256, 256)
                return blks[("g", g)], blks[("u", g)]

            def get_d(g):
                if ("d", g) not in blks:
                    blks[("d", g)] = wrows(pre + "_w_down", g * 256, 256)
                return blks[("d", g)]

            hid_of = {}

            def GU(g, s):
                bg, bu = get_gu(g)
                hb = next_of("hid", 2)
                hid_of[(g, s)] = hb
                for f in range(2):
                    pg = next_ps()
                    pu = next_ps()
                    for (blk, pb) in ((bg, pg), (bu, pu)):
                        for k in range(DC):
                            S.add("pe", lambda eng, pb=pb, k=k, s=s, lhsT=blk.ap[:, k, f * 128:(f + 1) * 128]: eng.matmul(
                                PS[:, pb, :], lhsT=lhsT,
                                rhs=XN[:, k, s * 512:(s + 1) * 512],
                                start=(k == 0), stop=(k == DC - 1)),
                                reads=[blk.res, r_XN[k][s]], writes=[r_ps[pb]])
                    t = next_of("tmp", NTMP)
                    S.add("act", lambda eng, pg=pg, t=t: eng.activation(
                        out=TMP[:, t, 0:512], in_=PS[:, pg, :], func=AF.Silu),
                        reads=[r_ps[pg]], writes=[r_tmp[t]])
                    S.add("dve", lambda eng, pu=pu, t=t, hb=hb, f=f: eng.tensor_tensor(
                        out=HID[:, hb, f, :], in0=PS[:, pu, :], in1=TMP[:, t, 0:512], op=ALU.mult),
                        reads=[r_ps[pu], r_tmp[t]], writes=[r_hid[hb]])

            def DN(g, s):
                bd = get_d(g)
                hb = hid_of[(g, s)]
                for j in range(DC):
                    pb = next_ps()
                    for f in range(2):
                        S.add("pe", lambda eng, pb=pb, f=f, hb=hb, lhsT=bd.ap[:, f, j * 128:(j + 1) * 128]: eng.matmul(
                            PS[:, pb, :], lhsT=lhsT,
                            rhs=HID[:, hb, f, :], start=(f == 0), stop=(f == 1)),
                            reads=[bd.res, r_hid[hb]], writes=[r_ps[pb]])
                    S.add("dve", lambda eng, pb=pb, j=j, s=s: eng.scalar_tensor_tensor(
                        out=H[:, j, s * 512:(s + 1) * 512], in0=PS[:, pb, :], scalar=0.5,
                        in1=H[:, j, s * 512:(s + 1) * 512], op0=ALU.mult, op1=ALU.add),
                        reads=[r_ps[pb], r_H[j][s]], writes=[r_H[j][s]])

            for idx, (g, s) in enumerate(items):
                GU(g, s)
                if idx == 0 and len(subs) > 1:
                    rmsnorm_to_xn(gcol, subs[1:], None, xn_full)
                if idx > 0:
                    DN(*items[idx - 1])
            DN(*items[-1])

        def load_x_chunk(j, col0, subs):
            n = len(subs)
            S.add("sp", lambda eng: eng.dma_start(
                out=H[:, j, 0:n * 512], in_=xT[j * 128:(j + 1) * 128, col0:col0 + n * 512]),
                writes=[r_H[j][s] for s in subs], dma_sem=sem_xj[j])

        def load_x(col0, subs):
            for j in range(DC):
                load_x_chunk(j, col0, subs)

        def store_chunk(tile, j):
            S.add("sp", lambda eng: eng.dma_start(
                out=outT[j * 128:(j + 1) * 128, tile * 1024:(tile + 1) * 1024], in_=H[:, j, :]),
                reads=[r_H[j][0], r_H[j][1]], dma_sem=sem_stj[j])

        def U_of(k):
            return XN[:, k, 0:512]

        def xn_u(j, s):
            return XN[:, j, 0:512], r_XN[j][0]

        def proj_fm(blk, col0, rhs_of, rres_of, nk, ncol=512, rcol0=0):
            b = next_ps()
            for k in range(nk):
                lhsT = blk.ap[:, k, col0:col0 + 128]
                rhs = rhs_of(k)
                S.add("pe", lambda eng, k=k, lhsT=lhsT, rhs=rhs: eng.matmul(
                    PS[:, b, 0:ncol], lhsT=lhsT, rhs=rhs,
                    start=(k == 0), stop=(k == nk - 1)),
                    reads=[blk.res, rres_of(k)], writes=[r_ps[b]])
            return b

        def mix(s, gs, halo_only=False, pre_normed=False, norm_next=None):
            cur = gs % 2
            prv = 1 - cur
            c0 = s * 512
            if not pre_normed:
                rmsnorm_to_xn(C_MIX, [s], None, xn_u)
            u_rhs = lambda k: U_of(k)
            u_res = lambda k: r_XN[k][0]

            pw = None
            for c in range(8):
                if c % 2 == 0:
                    zblk = wcols("w_in", (c // 2) * 256, 256)
                if c == 2 and not halo_only:
                    pw_src = W["pool_w"].rearrange("g (kc p) n -> p (g kc) n", p=128)
                    pw = wload(pw_src, (8, 256))
                g = c // 2
                w = POOL_WINDOWS[g]
                if halo_only:
                    b = next_ps()
                    for k in range(DC):
                        S.add("pe", lambda eng, k=k, b=b, lhsT=zblk.ap[:, k, (c % 2) * 128:(c % 2) * 128 + 128]: eng.matmul(
                            PS[:, b, 0:16], lhsT=lhsT,
                            rhs=XN[:, k, 496:512], start=(k == 0), stop=(k == DC - 1)),
                            reads=[zblk.res, r_XN[k][0]], writes=[r_ps[b]])
                    S.add("act", lambda eng, b=b, c=c: eng.activation(
                        out=ZT[:, c, :], in_=PS[:, b, 0:16], func=AF.Copy),
                        reads=[r_ps[b]], writes=[r_ZT[c]])
                    continue
                b = proj_fm(zblk, (c % 2) * 128, u_rhs, u_res, DC)
                tz = next_of("tmp", NTMP)
                S.add("dve", lambda eng, tz=tz, c=c: eng.tensor_copy(out=TMP[:, tz, 0:16], in_=ZT[:, c, :]),
                      reads=[r_ZT[c]], writes=[r_tmp[tz]])
                S.add("act", lambda eng, tz=tz, b=b: eng.activation(
                    out=TMP[:, tz, 16:528], in_=PS[:, b, :], func=AF.Copy),
                    reads=[r_ps[b]], writes=[r_tmp[tz]])
                S.add("dve", lambda eng, tz=tz, c=c: eng.tensor_copy(out=ZT[:, c, :], in_=TMP[:, tz, 512:528]),
                      reads=[r_tmp[tz]], writes=[r_ZT[c]])
                tc_ = tz
                n = 1
                while n < w:
                    tn = next_of("tmp", NTMP)
                    assert tn != tz
                    S.add("dve", lambda eng, tn=tn, tc_=tc_, n=n: eng.tensor_tensor(
                        out=TMP[:, tn, 2 * n - 1:528], in0=TMP[:, tc_, 2 * n - 1:528],
                        in1=TMP[:, tc_, n - 1:528 - n], op=ALU.add),
                        reads=[r_tmp[tc_]], writes=[r_tmp[tn]])
                    tc_ = tn
                    n *= 2
                S.add("dve", lambda eng, tc_=tc_, tz=tz, c=c, w=w: eng.scalar_tensor_tensor(
                    out=YA[:, c, :], in0=TMP[:, tc_, 16:528], scalar=1.0 / w, in1=TMP[:, tz, 16:528],
                    op0=ALU.mult, op1=ALU.subtract),
                    reads=[r_tmp[tc_], r_tmp[tz]], writes=[r_YA[c]])
                if gs == 1:
                    i = next_of("rd", 2)
                    S.add("dve", lambda eng, tc_=tc_, i=i, g=g: eng.tensor_tensor(
                        out=RD[:, i, 0:16], in0=TMP[:, tc_, 16:32],
                        in1=CON[:, C_INVC + g * 16:C_INVC + (g + 1) * 16], op=ALU.mult),
                        reads=[r_tmp[tc_], r_con], writes=[r_rd[i]])
                    S.add("dve", lambda eng, tz=tz, i=i, c=c: eng.tensor_tensor(
                        out=YA[:, c, 0:16], in0=RD[:, i, 0:16], in1=TMP[:, tz, 16:32], op=ALU.subtract),
                        reads=[r_rd[i], r_tmp[tz]], writes=[r_YA[c]])

                def group_mm(g):
                    for o in range(2):
                        b = next_ps()
                        for ci in range(2):
                            cc = 2 * g + ci
                            S.add("pe", lambda eng, b=b, ci=ci, cc=cc, lhsT=pw.ap[:, g * 2 + ci, o * 128:(o + 1) * 128]: eng.matmul(
                                PS[:, b, :], lhsT=lhsT,
                                rhs=YA[:, cc, :], start=(ci == 0), stop=(ci == 1)),
                                reads=[pw.res, r_YA[cc]], writes=[r_ps[b]])
                        idx = 2 * g + o
                        S.add("dve", lambda eng, b=b, idx=idx: eng.tensor_scalar(
                            out=YP[:, idx, :], in0=PS[:, b, :], scalar1=CON[:, C_PSC + idx:C_PSC + idx + 1],
                            scalar2=None, op0=ALU.mult),
                            reads=[r_ps[b], r_con], writes=[r_YP[idx]])
                if c % 2 == 1:
                    if g > 0:
                        group_mm(g - 1)
                    if g == 3:
                        group_mm(3)

            bias_blk = {}
            for pr in range(4):
                if not halo_only:
                    qblk = wcols("w_in", 1024 + pr * 256, 256)
                kblk = wcols("w_in", 2048 + pr * 256, 256)
                vblk = wcols("w_in", 3072 + pr * 256, 256)
                if not halo_only:
                    bsrc = biasT_d[:, pr * 1280:(pr + 1) * 1280].rearrange("p (a b) -> p a b", a=2)
                    bias_cur = wload(bsrc, (2, 640))
                pend = []
                if not halo_only:
                    for hh in range(2):
                        pend.append(("q", hh, proj_fm(qblk, hh * 128, u_rhs, u_res, DC)))
                for hh in range(2):
                    pend.append(("k", hh, proj_fm(kblk, hh * 128, u_rhs, u_res, DC)))
                for (kind, hh, b) in pend:
                    hd = 2 * pr + hh
                    q = next_of("sq", 2)
                    S.add("act", lambda eng, b=b, q=q: eng.activation(
                        out=SQ[:, q, :], in_=PS[:, b, :], func=AF.Square),
                        reads=[r_ps[b]], writes=[r_sq[q]])
                    b2 = next_ps()
                    ones = ONES_1 if kind == "q" else ONES_H
                    S.add("pe", lambda eng, b2=b2, q=q, ones=ones: eng.matmul(
                        PS[:, b2, :], lhsT=ones[:, :], rhs=SQ[:, q, :], start=True, stop=True),
                        reads=[r_sq[q], r_misc], writes=[r_ps[b2]])
                    t = next_of("tmp", NTMP)
                    rstd_from_ps(b2, t, 1 if kind == "q" else 0)
                    if kind == "q":
                        dst, rdst, gcol = HID[:, hh, 0, :], r_hid[hh], C_QN
                    else:
                        dst, rdst, gcol = KB[:, hd, cur * 512:(cur + 1) * 512], r_KB[hd][cur], C_KN
                    S.add("dve", lambda eng, b=b, t=t, dst=dst, gcol=gcol: eng.scalar_tensor_tensor(
                        out=dst, in0=PS[:, b, :], scalar=CON[:, gcol:gcol + 1], in1=TMP[:, t, 0:512],
                        op0=ALU.mult, op1=ALU.mult),
                        reads=[r_ps[b], r_tmp[t], r_con], writes=[rdst])
                for tb2 in range(2):
                    b = next_ps()
                    for half in range(2):
                        tb = tb2 * 2 + half
                        for k in range(DC):
                            S.add("pe", lambda eng, b=b, half=half, tb=tb, k=k, rhs=vblk.ap[:, k, :]: eng.matmul(
                                PS[:, b, half * 256:(half + 1) * 256],
                                lhsT=XN[:, k, tb * 128:(tb + 1) * 128], rhs=rhs,
                                start=(k == 0), stop=(k == DC - 1)),
                                reads=[vblk.res, r_XN[k][0]], writes=[r_ps[b]])
                    for half in range(2):
                        tb = tb2 * 2 + half
                        S.add("act", lambda eng, b=b, half=half, tb=tb, pr=pr: eng.activation(
                            out=VB[:, cur * 4 + tb, pr * 256:(pr + 1) * 256],
                            in_=PS[:, b, half * 256:(half + 1) * 256], func=AF.Copy),
                            reads=[r_ps[b]], writes=[r_VB[tb][cur][pr]])
                if halo_only:
                    continue
                def att_scores(hh, j):
                    hd = 2 * pr + hh
                    bA = next_ps()
                    bB = next_ps()
                    for r in range(5):
                        L = j + r
                        hf = prv if L < 4 else cur
                        kcol = hf * 512 + (L % 4) * 128
                        pb, pc = (bA, r * 128) if r < 4 else (bB, 0)
                        need_mask = (gs == 1 and L < 4)
                        S.add("pe", lambda eng, pb=pb, pc=pc, hd=hd, hh=hh, kcol=kcol, j=j: eng.matmul(
                            PS[:, pb, pc:pc + 128], lhsT=KB[:, hd, kcol:kcol + 128],
                            rhs=HID[:, hh, 0, j * 128:(j + 1) * 128], start=True, stop=False),
                            reads=[r_KB[hd][hf], r_hid[hh]], writes=[r_ps[pb]])
                        S.add("pe", lambda eng, pb=pb, pc=pc, need_mask=need_mask, rhs=bias_cur.ap[:, hh, r * 128:(r + 1) * 128]: eng.matmul(
                            PS[:, pb, pc:pc + 128], lhsT=IDENT[:, :],
                            rhs=rhs, start=False, stop=(not need_mask)),
                            reads=[bias_cur.res, r_ident], writes=[r_ps[pb]])
                        if need_mask:
                            S.add("pe", lambda eng, pb=pb, pc=pc: eng.matmul(
                                PS[:, pb, pc:pc + 128], lhsT=IDENT[:, :], rhs=HNEG[:, :],
                                start=False, stop=True),
                                reads=[r_hneg, r_ident], writes=[r_ps[pb]])
                    pi = next_of("pt", 3)
                    S.add("act", lambda eng, bA=bA, pi=pi: eng.activation(
                        out=PT[:, pi, 0:512], in_=PS[:, bA, :], func=AF.Exp),
                        reads=[r_ps[bA]], writes=[r_pt[pi]])
                    S.add("act", lambda eng, bB=bB, pi=pi: eng.activation(
                        out=PT[:, pi, 512:640], in_=PS[:, bB, 0:128], func=AF.Exp),
                        reads=[r_ps[bB]], writes=[r_pt[pi]])
                    return pi

                def att_pv(hh, j, pi):
                    hd = 2 * pr + hh
                    bO = next_ps()
                    for r in range(5):
                        L = j + r
                        hf = prv if L < 4 else cur
                        S.add("pe", lambda eng, bO=bO, r=r, L=L, hf=hf, hd=hd, pi=pi: eng.matmul(
                            PS[:, bO, 0:128], lhsT=VB[:, hf * 4 + (L % 4), hd * 128:(hd + 1) * 128],
                            rhs=PT[:, pi, r * 128:(r + 1) * 128], start=(r == 0), stop=(r == 4)),
                            reads=[r_VB[L % 4][hf][pr], r_pt[pi]], writes=[r_ps[bO]])
                    for r in range(5):
                        S.add("pe", lambda eng, bO=bO, r=r, pi=pi: eng.matmul(
                            PS[:, bO, 128:256], lhsT=ONES_1[:, :],
                            rhs=PT[:, pi, r * 128:(r + 1) * 128], start=(r == 0), stop=(r == 4)),
                            reads=[r_pt[pi], r_misc], writes=[r_ps[bO]])
                    i = next_of("rd", 2)
                    S.add("dve", lambda eng, bO=bO, i=i: eng.reciprocal(out=RD[:, i, :], in_=PS[:, bO, 128:256]),
                          reads=[r_ps[bO]], writes=[r_rd[i]])
                    S.add("dve", lambda eng, bO=bO, i=i, hd=hd, j=j: eng.tensor_tensor(
                        out=YA[:, hd, j * 128:(j + 1) * 128], in0=PS[:, bO, 0:128], in1=RD[:, i, :],
                        op=ALU.mult),
                        reads=[r_ps[bO], r_rd[i]], writes=[r_YA[hd]])

                aitems = [(hh, j) for hh in range(2) for j in range(4)]
                pend_pv = []
                for it in aitems:
                    pi = att_scores(*it)
                    pend_pv.append((it[0], it[1], pi))
                    if len(pend_pv) > 2:
                        att_pv(*pend_pv.pop(0))
                while pend_pv:
                    att_pv(*pend_pv.pop(0))
            if halo_only:
                return

            for jp in range(8):
                tt = {}
                wga = wcols("w_branch_gate", jp * 256, 256)
                for o in range(2):
                    j = 2 * jp + o
                    bG = proj_fm(wga, o * 128, u_rhs, u_res, DC)
                    t1 = next_of("tmp", NTMP)
                    tt[("a", o)] = t1
                    S.add("act", lambda eng, bG=bG, t1=t1, j=j: eng.activation(
                        out=TMP[:, t1, 0:512], in_=PS[:, bG, :], func=AF.Sigmoid,
                        bias=CON[:, C_BG + j:C_BG + j + 1]),
                        reads=[r_ps[bG], r_con], writes=[r_tmp[t1]])
                wgb = wcols("w_branch_gate", D + jp * 256, 256)
                for o in range(2):
                    j = 2 * jp + o
                    bG = proj_fm(wgb, o * 128, u_rhs, u_res, DC)
                    t2 = next_of("tmp", NTMP)
                    tt[("b", o)] = t2
                    S.add("act", lambda eng, bG=bG, t2=t2, j=j: eng.activation(
                        out=TMP[:, t2, 0:512], in_=PS[:, bG, :], func=AF.Sigmoid,
                        bias=CON[:, C_BG + 16 + j:C_BG + 16 + j + 1]),
                        reads=[r_ps[bG], r_con], writes=[r_tmp[t2]])
                wa = wcols("w_br_pool", jp * 256, 256)
                for o in range(2):
                    bY = proj_fm(wa, o * 128, lambda k: YP[:, k, :], lambda k: r_YP[k], 8)
                    t1 = tt[("a", o)]
                    S.add("dve", lambda eng, bY=bY, t1=t1: eng.tensor_tensor(
                        out=TMP[:, t1, 0:512], in0=PS[:, bY, :], in1=TMP[:, t1, 0:512], op=ALU.mult),
                        reads=[r_ps[bY], r_tmp[t1]], writes=[r_tmp[t1]])
                wb = wcols("w_br_attn", jp * 256, 256)
                for o in range(2):
                    j = 2 * jp + o
                    bY = proj_fm(wb, o * 128, lambda k: YA[:, k, :], lambda k: r_YA[k], 8)
                    t1 = tt[("a", o)]
                    t2 = tt[("b", o)]
                    S.add("dve", lambda eng, bY=bY, t2=t2: eng.tensor_tensor(
                        out=TMP[:, t2, 0:512], in0=PS[:, bY, :], in1=TMP[:, t2, 0:512], op=ALU.mult),
                        reads=[r_ps[bY], r_tmp[t2]], writes=[r_tmp[t2]])
                    S.add("dve", lambda eng, t1=t1, t2=t2, j=j: eng.tensor_tensor(
                        out=XN[:, j, 512:1024], in0=TMP[:, t1, 0:512], in1=TMP[:, t2, 0:512], op=ALU.add),
                        reads=[r_tmp[t1], r_tmp[t2]], writes=[r_XN[j][1]])

            if norm_next is not None:
                rmsnorm_to_xn(C_MIX, [norm_next], None, xn_u)

            for ip in range(8):
                wo = wcols("w_out", ip * 256, 256)
                for o in range(2):
                    i = 2 * ip + o
                    b = proj_fm(wo, o * 128, lambda k: XN[:, k, 512:1024], lambda k: r_XN[k][1], DC)
                    S.add("dve", lambda eng, b=b, i=i: eng.tensor_tensor(
                        out=H[:, i, c0:c0 + 512], in0=PS[:, b, :], in1=H[:, i, c0:c0 + 512], op=ALU.add),
                        reads=[r_ps[b], r_H[i][s]], writes=[r_H[i][s]])

        def ple(tile):
            subs = [0, 1]
            rmsnorm_to_xn(C_PLE, subs, None, xn_full)
            psrc = pT[:, tile * 1024:(tile + 1) * 1024].rearrange("(kc p) n -> p kc n", p=128)
            pview = YP[:, 0:4, :].rearrange("p a b -> p (a b)").rearrange("p (a b) -> p a b", a=2)
            wview = YA[:, :, :].rearrange("p a b -> p (a b)").rearrange("p (a b) -> p a b", a=2)
            wsrc = W["w_ple"].rearrange("(f p) n -> p f n", p=128)
            S.add("pool", lambda eng: eng.dma_start(out=pview, in_=psrc),
                  writes=r_YP, dma_sem=sem_p, batch=True)
            S.add("pool", lambda eng: eng.dma_start(out=wview, in_=wsrc),
                  writes=r_YA, dma_sem=sem_p, batch=True)
            S.end_batch(sem_p)
            for jp in range(8):
                wpg = wcols("w_ple_gate", jp * 256, 256)
                for o in range(2):
                    j = 2 * jp + o
                    for s in subs:
                        bG = proj_fm(wpg, o * 128, lambda k: XN[:, k, s * 512:(s + 1) * 512],
                                     lambda k: r_XN[k][s], DC)
                        bE = next_ps()
                        for kc in range(2):
                            S.add("pe", lambda eng, bE=bE, kc=kc, j=j, s=s: eng.matmul(
                                PS[:, bE, :], lhsT=wview[:, kc, j * 128:(j + 1) * 128],
                                rhs=pview[:, kc, s * 512:(s + 1) * 512], start=(kc == 0), stop=(kc == 1)),
                                reads=r_YP + r_YA, writes=[r_ps[bE]])
                        t = next_of("tmp", NTMP)
                        S.add("act", lambda eng, bG=bG, t=t: eng.activation(
                            out=TMP[:, t, 0:512], in_=PS[:, bG, :], func=AF.Sigmoid),
                            reads=[r_ps[bG]], writes=[r_tmp[t]])
                        S.add("dve", lambda eng, bE=bE, t=t: eng.tensor_tensor(
                            out=TMP[:, t, 0:512], in0=PS[:, bE, :], in1=TMP[:, t, 0:512], op=ALU.mult),
                            reads=[r_ps[bE], r_tmp[t]], writes=[r_tmp[t]])
                        S.add("dve", lambda eng, t=t, j=j, s=s: eng.tensor_tensor(
                            out=H[:, j, s * 512:(s + 1) * 512], in0=H[:, j, s * 512:(s + 1) * 512],
                            in1=TMP[:, t, 0:512], op=ALU.add),
                            reads=[r_tmp[t], r_H[j][s]], writes=[r_H[j][s]])
                    store_chunk(tile, j)
                    if tile + 1 < ntiles:
                        load_x_chunk(j, HALO + (tile + 1) * 1024, [0, 1])

        gs = 0
        if "mix" in stages:
            load_x(0, [0])
            if "ffn1" in stages:
                ffn("ffn1", C_FFN1, [0])
            mix(0, 0, halo_only=True)
        for tile in range(ntiles):
            if tile == 0 or "ple" not in stages:
                load_x(HALO + tile * 1024, [0, 1])
            if "ffn1" in stages:
                ffn("ffn1", C_FFN1, [0, 1])
            if "mix" in stages:
                for s in range(2):
                    gs += 1
                    mix(s, gs, pre_normed=(s == 1), norm_next=(1 if s == 0 else None))
            if "ffn2" in stages:
                ffn("ffn2", C_FFN2, [0, 1])
            if "ple" in stages:
                ple(tile)
            else:
                for j in range(DC):
                    store_chunk(tile, j)
        for j in range(DC):
            S.add("sp", lambda eng, j=j, v=S.dma_counts.get(id(sem_stj[j]), 0): eng.wait_ge(sem_stj[j], v))

        block = es.enter_context(nc.Block())
        S.emit(nc, block, prog)
    return nc


_W_NAMES = ["ffn1_w_gate", "ffn1_w_up", "ffn1_w_down", "ffn2_w_gate", "ffn2_w_up", "ffn2_w_down",
            "w_in", "pool_w", "w_br_pool", "w_br_attn", "w_branch_gate", "w_out", "w_ple_gate", "w_ple"]


def make_in_maps(inputs):
    f = lambda a: np.ascontiguousarray(np.asarray(a, dtype=np.float32))
    x = f(inputs["x"])[0]
    p = f(inputs["p"])[0, 0]
    xT_full = np.ascontiguousarray(x.T)
    pT_full = np.ascontiguousarray(p.T)
    shared = {n: f(inputs[n])[0] for n in _W_NAMES}
    con = np.zeros((128, NCONST), np.float32)
    lay = lambda v: f(v).reshape(-1, 128).T
    con[:, C_FFN1:C_FFN1 + 16] = lay(inputs["ffn1_norm"][0])
    con[:, C_MIX:C_MIX + 16] = lay(inputs["mix_norm"][0])
    con[:, C_FFN2:C_FFN2 + 16] = lay(inputs["ffn2_norm"][0])
    con[:, C_PLE:C_PLE + 16] = lay(inputs["ple_norm"][0])
    con[:, C_BG:C_BG + 32] = lay(inputs["b_branch_gate"][0])
    con[:, C_PSC:C_PSC + 8] = lay(inputs["pool_scale"][0])
    con[:, C_QN] = f(inputs["q_norm"][0])
    con[:, C_KN] = f(inputs["k_norm"][0])
    rb = f(inputs["rel_bias"][0])
    kj = np.arange(640)[:, None]
    qi = np.arange(128)[None, :]
    dist = qi - kj + 512
    idx = np.clip(dist, -256, 256) + 256
    g = rb[:, idx]
    valid = np.where(qi < 64, kj < 576, kj >= 64)
    g = np.where(valid[None], g, np.float32(NEG)).astype(np.float32)
    biasT = np.ascontiguousarray(g.reshape(NH, 5, 128, 128).transpose(2, 0, 1, 3)).reshape(128, NH * 5 * 128)
    in_maps = []
    for c in range(NCORES):
        m = dict(shared)
        xs = np.zeros((D, HALO + TOK), np.float32)
        lo = c * TOK - HALO
        if lo >= 0:
            xs[:, :] = xT_full[:, lo:lo + HALO + TOK]
        else:
            xs[:, HALO:] = xT_full[:, 0:TOK]
        m["xT"] = xs
        m["pT"] = np.ascontiguousarray(pT_full[:, c * TOK:(c + 1) * TOK])
        cc = con.copy()
        for gi, w in enumerate(POOL_WINDOWS):
            t = np.arange(16) + c * TOK
            cc[:, C_INVC + gi * 16:C_INVC + (gi + 1) * 16] = (1.0 / np.minimum(t + 1, w)).astype(np.float32)[None, :]
        m["consts"] = cc
        m["halo_neg"] = np.full((128, 128), NEG if c == 0 else 0.0, np.float32)
        m["biasT"] = biasT
        m["ident"] = np.eye(128, dtype=np.float32)
        in_maps.append(m)
    return in_maps


_NC_CACHE = {}


def kernel(**inputs):
    in_maps = make_in_maps(inputs)
    key = "full"
    if key not in _NC_CACHE:
        _NC_CACHE[key] = build_program()
    nc = _NC_CACHE[key]
    res = run_bass_kernel_spmd(nc, in_maps, core_ids=list(range(NCORES)))
    outs = [np.asarray(r["outT"]) for r in res.results]
    out = np.concatenate([o.T for o in outs], axis=0)
    return np.ascontiguousarray(out.astype(np.float32))[None]
```
